# Optimizing a Trainium2 kernel written in Bass

```python
import jax, jax.numpy as jnp
from jax import lax
import numpy as np

D_MODEL = 1024
BATCH = 4
SEQ = 4096
DEPTH = 2

N_MIXERS = 2
HEAD_DIM = 64
N_Q_HEADS = D_MODEL // HEAD_DIM
N_KV_HEADS = N_Q_HEADS // 4
GQA_GROUP = N_Q_HEADS // N_KV_HEADS
WINDOW = 128
BLOCK = 128
ROT_DIM = HEAD_DIM // 4
ROPE_THETA = 500000.0
LRU_WIDTH = D_MODEL
LRU_BLOCKS = 4
LRU_BLOCK_W = LRU_WIDTH // LRU_BLOCKS
CONV_WIDTH = 4
CONV_LEFT = 2
LRU_C = 8.0
N_EXPERTS = 16
EXPERT_FF = D_MODEL
CAPACITY_FACTOR = 2
LN_EPS = 1e-5
ALPHA = (2 * DEPTH) ** 0.25
BETA = (8 * DEPTH) ** -0.25
N_ATTN_LAYERS = (DEPTH + 1) // 2
N_LRU_LAYERS = DEPTH // 2

kernel_name = 'hybrid_swa_sink_rglru_ecmoe_deepnorm'


def layer_norm(x, g, b):
    xf = x.astype(jnp.float32)
    mu = jnp.mean(xf, axis=-1, keepdims=True)
    var = jnp.mean(jnp.square(xf - mu), axis=-1, keepdims=True)
    y = (xf - mu) * lax.rsqrt(var + LN_EPS) * g.astype(jnp.float32) + b.astype(jnp.float32)
    return y.astype(x.dtype)


def partial_rotary(t, cos, sin):
    tr, tp = t[..., :ROT_DIM], t[..., ROT_DIM:]
    t1, t2 = tr[..., :ROT_DIM // 2], tr[..., ROT_DIM // 2:]
    rot = jnp.concatenate([-t2, t1], axis=-1)
    return jnp.concatenate([tr * cos + rot * sin, tp], axis=-1)


def windowed_gqa_sink(x, w_qkv, w_o, sink):
    B, S, _ = x.shape
    nb = S // BLOCK
    qkv = x @ w_qkv
    q, k, v = jnp.split(qkv, [N_Q_HEADS * HEAD_DIM, (N_Q_HEADS + N_KV_HEADS) * HEAD_DIM], axis=-1)
    q = q.reshape(B, S, N_KV_HEADS, GQA_GROUP, HEAD_DIM)
    k = k.reshape(B, S, N_KV_HEADS, HEAD_DIM)
    v = v.reshape(B, S, N_KV_HEADS, HEAD_DIM)
    pos = jnp.arange(S, dtype=jnp.float32)
    inv_freq = ROPE_THETA ** (-jnp.arange(0, ROT_DIM, 2, dtype=jnp.float32) / ROT_DIM)
    ang = pos[:, None] * inv_freq[None, :]
    ang = jnp.concatenate([ang, ang], axis=-1)
    cos, sin = jnp.cos(ang).astype(x.dtype), jnp.sin(ang).astype(x.dtype)
    q = partial_rotary(q, cos[None, :, None, None, :], sin[None, :, None, None, :])
    k = partial_rotary(k, cos[None, :, None, :], sin[None, :, None, :])

    pad = ((0, 0), (BLOCK, BLOCK), (0, 0), (0, 0))
    kb = jnp.pad(k, pad).reshape(B, nb + 2, BLOCK, N_KV_HEADS, HEAD_DIM)
    vb = jnp.pad(v, pad).reshape(B, nb + 2, BLOCK, N_KV_HEADS, HEAD_DIM)
    k_band = jnp.concatenate([kb[:, :-2], kb[:, 1:-1], kb[:, 2:]], axis=2)
    v_band = jnp.concatenate([vb[:, :-2], vb[:, 1:-1], vb[:, 2:]], axis=2)
    qb = q.reshape(B, nb, BLOCK, N_KV_HEADS, GQA_GROUP, HEAD_DIM)

    s = jnp.einsum('bnqhgd,bnkhd->bnhgqk', qb, k_band).astype(jnp.float32) * (HEAD_DIM ** -0.5)
    blk = jnp.arange(nb)
    qpos = blk[:, None] * BLOCK + jnp.arange(BLOCK)[None, :]
    kpos = (blk[:, None] - 1) * BLOCK + jnp.arange(3 * BLOCK)[None, :]
    valid = ((jnp.abs(qpos[:, :, None] - kpos[:, None, :]) <= WINDOW)
             & (kpos[:, None, :] >= 0) & (kpos[:, None, :] < S))
    s = jnp.where(valid[None, :, None, None], s, -jnp.inf)
    sink_l = sink.astype(jnp.float32).reshape(N_KV_HEADS, GQA_GROUP)[None, None, :, :, None, None]
    m = jnp.maximum(jnp.max(s, axis=-1, keepdims=True), sink_l)
    p = jnp.exp(s - m)
    denom = jnp.sum(p, axis=-1, keepdims=True) + jnp.exp(sink_l - m)
    probs = (p / denom).astype(x.dtype)
    o = jnp.einsum('bnhgqk,bnkhd->bnqhgd', probs, v_band).reshape(B, S, D_MODEL)
    return o @ w_o


def _linear_combine(left, right):
    a1, b1 = left
    a2, b2 = right
    return a1 * a2, a2 * b1 + b2


def bidir_rglru_block(x, w_in, conv_w, conv_b, w_rgate, b_rgate, w_igate, b_igate, lam, w_out):
    B, S, _ = x.shape
    gate_br, xb = jnp.split(x @ w_in, 2, axis=-1)
    xp = jnp.pad(xb, ((0, 0), (CONV_LEFT, CONV_WIDTH - 1 - CONV_LEFT), (0, 0)))
    xc = conv_b
    for j in range(CONV_WIDTH):
        xc = xc + conv_w[j] * xp[:, j:j + S]
    xr = xc.reshape(B, S, LRU_BLOCKS, LRU_BLOCK_W)
    r = jax.nn.sigmoid(jnp.einsum('bsnc,zncf->zbsnf', xr, w_rgate).reshape(2, B, S, LRU_WIDTH).astype(jnp.float32)
                       + b_rgate.astype(jnp.float32)[:, None, None, :])
    i = jax.nn.sigmoid(jnp.einsum('bsnc,zncf->zbsnf', xr, w_igate).reshape(2, B, S, LRU_WIDTH).astype(jnp.float32)
                       + b_igate.astype(jnp.float32)[:, None, None, :])
    log_a = -LRU_C * r * jax.nn.softplus(-lam.astype(jnp.float32))[:, None, None, :]
    a = jnp.exp(log_a)
    mult = jnp.sqrt(-jnp.expm1(2.0 * log_a))
    t = jnp.arange(S)
    is_start = jnp.stack([t == 0, t == S - 1])
    mult = jnp.where(is_start[:, None, :, None], 1.0, mult)
    u = mult * i * xc.astype(jnp.float32)[None]
    _, h_f = lax.associative_scan(_linear_combine, (a[0], u[0]), axis=1)
    _, h_b = lax.associative_scan(_linear_combine, (a[1], u[1]), reverse=True, axis=1)
    h = (h_f + h_b).astype(x.dtype)
    return (h * jax.nn.gelu(gate_br)) @ w_out


def expert_choice_moe(x, w_router, w_gate_up, w_down):
    B, S, _ = x.shape
    cap = CAPACITY_FACTOR * S // N_EXPERTS
    aff = jax.nn.softmax((x @ w_router).astype(jnp.float32), axis=-1)
    gates, idx = lax.top_k(jnp.swapaxes(aff, 1, 2), cap)
    bi = jnp.arange(B)[:, None, None]
    xs = x[bi, idx]
    g, u = jnp.split(jnp.einsum('becd,edf->becf', xs, w_gate_up), 2, axis=-1)
    out = jnp.einsum('becf,efd->becd', jax.nn.silu(g) * u, w_down) * gates[..., None].astype(x.dtype)
    return jnp.zeros_like(x).at[bi, idx].add(out)


def setup_inputs(seed: int = 0) -> dict:
    key = jax.random.key(seed)
    ks = jax.random.split(key, 24)
    f32 = jnp.float32
    nrm = lambda k, shape, scale: jax.random.normal(k, shape, f32) * scale
    qkv_out = (N_Q_HEADS + 2 * N_KV_HEADS) * HEAD_DIM
    a8 = jax.random.uniform(ks[11], (N_LRU_LAYERS, 2, LRU_WIDTH), f32, 0.9, 0.999)
    a_base = a8 ** (1.0 / LRU_C)
    lam = jnp.log(a_base) - jnp.log1p(-a_base)
    return {
        'x': jax.random.normal(ks[0], (BATCH, SEQ, D_MODEL), f32),
        'attn_w_qkv': nrm(ks[1], (N_ATTN_LAYERS, D_MODEL, qkv_out), D_MODEL ** -0.5),
        'attn_w_o': nrm(ks[2], (N_ATTN_LAYERS, D_MODEL, D_MODEL), BETA * D_MODEL ** -0.5),
        'attn_sink': nrm(ks[3], (N_ATTN_LAYERS, N_Q_HEADS), 0.5),
        'lru_w_in': nrm(ks[4], (N_LRU_LAYERS, D_MODEL, 2 * LRU_WIDTH), D_MODEL ** -0.5),
        'lru_conv_w': nrm(ks[5], (N_LRU_LAYERS, CONV_WIDTH, LRU_WIDTH), CONV_WIDTH ** -0.5),
        'lru_conv_b': nrm(ks[6], (N_LRU_LAYERS, LRU_WIDTH), 0.02),
        'lru_w_rgate': nrm(ks[7], (N_LRU_LAYERS, 2, LRU_BLOCKS, LRU_BLOCK_W, LRU_BLOCK_W), LRU_BLOCK_W ** -0.5),
        'lru_b_rgate': nrm(ks[8], (N_LRU_LAYERS, 2, LRU_WIDTH), 0.02),
        'lru_w_igate': nrm(ks[9], (N_LRU_LAYERS, 2, LRU_BLOCKS, LRU_BLOCK_W, LRU_BLOCK_W), LRU_BLOCK_W ** -0.5),
        'lru_b_igate': nrm(ks[10], (N_LRU_LAYERS, 2, LRU_WIDTH), 0.02),
        'lru_lambda': lam,
        'lru_w_out': nrm(ks[12], (N_LRU_LAYERS, LRU_WIDTH, D_MODEL), BETA * LRU_WIDTH ** -0.5),
        'moe_w_router': nrm(ks[13], (DEPTH, D_MODEL, N_EXPERTS), D_MODEL ** -0.5),
        'moe_w_gate_up': nrm(ks[14], (DEPTH, N_EXPERTS, D_MODEL, 2 * EXPERT_FF), D_MODEL ** -0.5),
        'moe_w_down': nrm(ks[15], (DEPTH, N_EXPERTS, EXPERT_FF, D_MODEL), BETA * EXPERT_FF ** -0.5),
        'ln_mix_g': 1.0 + nrm(ks[16], (DEPTH, D_MODEL), 0.02),
        'ln_mix_b': nrm(ks[17], (DEPTH, D_MODEL), 0.02),
        'ln_ffn_g': 1.0 + nrm(ks[18], (DEPTH, D_MODEL), 0.02),
        'ln_ffn_b': nrm(ks[19], (DEPTH, D_MODEL), 0.02),
    }


def reference(x, attn_w_qkv, attn_w_o, attn_sink, lru_w_in, lru_conv_w, lru_conv_b,
              lru_w_rgate, lru_b_rgate, lru_w_igate, lru_b_igate, lru_lambda, lru_w_out,
              moe_w_router, moe_w_gate_up, moe_w_down, ln_mix_g, ln_mix_b, ln_ffn_g, ln_ffn_b):
    for layer in range(DEPTH):
        j = layer // N_MIXERS
        if layer % N_MIXERS == 0:
            mix = windowed_gqa_sink(x, attn_w_qkv[j], attn_w_o[j], attn_sink[j])
        else:
            mix = bidir_rglru_block(x, lru_w_in[j], lru_conv_w[j], lru_conv_b[j],
                                    lru_w_rgate[j], lru_b_rgate[j], lru_w_igate[j], lru_b_igate[j],
                                    lru_lambda[j], lru_w_out[j])
        x = layer_norm(ALPHA * x + mix, ln_mix_g[layer], ln_mix_b[layer])
        ffn = expert_choice_moe(x, moe_w_router[layer], moe_w_gate_up[layer], moe_w_down[layer])
        x = layer_norm(ALPHA * x + ffn, ln_ffn_g[layer], ln_ffn_b[layer])
    return x
```

```python
import contextlib
import numpy as np
import concourse.bass as bass
import concourse.mybir as mybir
from concourse.bass_utils import run_bass_kernel_spmd

F32 = mybir.dt.float32
BF16 = mybir.dt.bfloat16
I32 = mybir.dt.int32
AF = mybir.ActivationFunctionType
ALU = mybir.AluOpType
AX = mybir.AxisListType

S = 4096
D = 1024
NT = 32
KD = 8
NE = 16
CAP = 512
ALPHA = float(4.0 ** 0.25)
EPS = 1e-5
NEG = -30000.0
NSLOT = 16
COMPUTE = ('pe', 'act', 'dve', 'pool')
QUEUES = ('pe', 'act', 'dve', 'pool', 'sp')


class Prog:
    def __init__(self, nc):
        self.nc = nc
        self.es = contextlib.ExitStack()
        self.q = {e: [] for e in QUEUES}
        self.cnt = {e: 0 for e in COMPUTE}
        self.known = {e: {} for e in QUEUES}
        self.res = {}
        self.dq = {'sp': 0, 'pool': 0}
        self.last_dma = {}
        self.sems = {}
        for e in COMPUTE:
            self.sems[('c', e)] = self.es.enter_context(nc.semaphore('s_' + e))
        for qn in ('sp', 'pool'):
            for s in range(NSLOT):
                self.sems[('d', qn, s)] = self.es.enter_context(nc.semaphore('d_%s_%d' % (qn, s)))
        self.ninst = 0

    def _need(self, eng, toks, same_ok):
        kn = self.known[eng]
        m = {}
        for t in toks:
            if t is None:
                continue
            if t[0] == 'c':
                if t[1] == eng and not same_ok:
                    continue
                key, v = ('c', t[1]), t[2]
            else:
                key, v = ('d', t[1], t[2]), t[3]
            if kn.get(key, 0) >= v:
                continue
            if m.get(key, 0) < v:
                m[key] = v
        for k, v in m.items():
            kn[k] = v
        return list(m.items())

    def _deps(self, reads, writes):
        raw, war = [], []
        for k in reads:
            r = self.res.get(k)
            if r and r[0]:
                raw.append(r[0])
        for k in writes:
            r = self.res.get(k)
            if r:
                if r[0]:
                    raw.append(r[0])
                war.extend(r[1].values())
        return raw, war

    def _commit(self, tok, reads, writes):
        kk = tok[:2] if tok[0] == 'c' else tok[:3]
        for k in reads:
            r = self.res.setdefault(k, [None, {}])
            r[1][kk] = tok
        for k in writes:
            self.res[k] = [tok, {}]

    def op(self, eng, fn, reads=(), writes=()):
        raw, war = self._deps(reads, writes)
        waits = self._need(eng, raw, eng != 'pe') + self._need(eng, war, False)
        self.cnt[eng] += 1
        tok = ('c', eng, self.cnt[eng])
        self.q[eng].append((waits, fn, ('c', eng)))
        self._commit(tok, reads, writes)
        self.ninst += 1
        return tok

    def dma(self, qn, fn, reads=(), writes=()):
        raw, war = self._deps(reads, writes)
        i = self.dq[qn]
        self.dq[qn] += 1
        slot = i % NSLOT
        val = 16 * (i // NSLOT + 1)
        prev = ('d', qn, slot, val - 16) if val > 16 else None
        waits = self._need(qn, raw + war + [prev], True)
        tok = ('d', qn, slot, val)
        self.q[qn].append((waits, fn, ('d', qn, slot)))
        self._commit(tok, reads, writes)
        self.last_dma[(qn, slot)] = tok
        self.ninst += 1
        return tok

    def barrier(self):
        toks = [('c', e, self.cnt[e]) for e in COMPUTE if self.cnt[e] > 0] + list(self.last_dma.values())
        for eng in QUEUES:
            waits = self._need(eng, toks, True)
            if waits:
                self.q[eng].append((waits, None, None))
        self.res = {}

    def flush(self):
        nc = self.nc
        sems = self.sems
        with nc.Block() as block:
            def run(engobj, items):
                for waits, fn, inc in items:
                    for k, v in waits:
                        engobj.wait_ge(sems[k], v)
                    if fn is None:
                        continue
                    ins = fn(engobj)
                    ins.then_inc(sems[inc], 1 if inc[0] == 'c' else 16)

            @block.sync
            def _(e):
                run(e, self.q['sp'])

            @block.tensor
            def _(e):
                run(e, self.q['pe'])

            @block.scalar
            def _(e):
                run(e, self.q['act'])

            @block.vector
            def _(e):
                run(e, self.q['dve'])

            @block.gpsimd
            def _(e):
                run(e, self.q['pool'])
        self.q = {e: [] for e in QUEUES}


class K:
    def __init__(self, dbg=False, stop=99):
        self.dbg = dbg
        self.stop = stop
        self.nc = nc = bass.Bass("TRN2", target_bir_lowering=False)
        self.P = Prog(nc)
        self.g = contextlib.ExitStack()
        self.d = {}
        self.outs = []

    def MM(self, out, lhsT, rhs, start, stop, r, w):
        self.P.op('pe', lambda e: e.matmul(out, lhsT, rhs, start=start, stop=stop), r, w)

    def TR(self, out, in_, ident, r, w):
        self.P.op('pe', lambda e: e.transpose(out, in_, ident), r, w)

    def ACT(self, out, in_, func, r, w, **kw):
        self.P.op('act', lambda e: e.activation(out, in_, func, **kw), r, w)

    def OP(self, eng, meth, args, r, w, **kw):
        self.P.op(eng, lambda e: getattr(e, meth)(*args, **kw), r, w)

    def DMA(self, q, out, in_, r, w):
        self.P.dma(q, lambda e: e.dma_start(out=out, in_=in_), r, w)

    def din(self, name, shape, dt=F32):
        t = self.nc.dram_tensor(name, list(shape), dt, kind="ExternalInput").ap()
        self.d[name] = t
        return t

    def dscr(self, name, shape, dt=F32, out=False):
        kind = "ExternalOutput" if (out or self.dbg) else "Internal"
        t = self.nc.dram_tensor(name, list(shape), dt, kind=kind).ap()
        self.d[name] = t
        if kind == "ExternalOutput":
            self.outs.append(name)
        return t

    def _sb(self, es):
        self.ph = getattr(self, 'ph', 0) + 1
        sfx = '_p%d' % self.ph
        return lambda name, shape, dt=F32: es.enter_context(self.nc.sbuf_tensor(name + sfx, list(shape), dt))

    def gsb(self, name, shape, dt=F32):
        return self.g.enter_context(self.nc.sbuf_tensor(name, list(shape), dt))

    def declare(self):
        nc = self.nc
        self.din('x', [S, D])
        self.din('attn_w_qkv', [1, D, 1536]); self.din('attn_w_o', [1, D, D]); self.din('attn_sink', [1, 16])
        self.din('lru_w_in', [1, D, 2048]); self.din('lru_conv_w', [1, 4, D]); self.din('lru_conv_b', [1, D])
        self.din('lru_w_rgate', [1, 2, 4, 256, 256]); self.din('lru_b_rgate', [1, 2, D])
        self.din('lru_w_igate', [1, 2, 4, 256, 256]); self.din('lru_b_igate', [1, 2, D])
        self.din('lru_lambda', [1, 2, D]); self.din('lru_w_out', [1, D, D])
        self.din('moe_w_router', [2, D, NE]); self.din('moe_w_gate_up', [2, NE, D, 2048]); self.din('moe_w_down', [2, NE, D, D])
        self.din('ln_mix_g', [2, D]); self.din('ln_mix_b', [2, D]); self.din('ln_ffn_g', [2, D]); self.din('ln_ffn_b', [2, D])
        self.din('c_ident', [128, 128]); self.din('c_cos', [S, 8]); self.din('c_sin', [S, 8])
        self.din('c_mask', [128, 384]); self.din('c_slot', [128, 4]); self.din('c_sel', [16, 2048])
        self.dscr('ybuf', [S, D]); self.dscr('x1b', [S, D], BF16); self.dscr('aff', [S, NE])
        self.dscr('x2buf', [S, D]); self.dscr('x2T', [D, S], BF16); self.dscr('hgT', [D, S], BF16)
        self.dscr('out', [S, D], out=True)
        if self.dbg:
            self.dscr('dbg_idx', [128, 64], I32)
        self.psA = self.g.enter_context(nc.psum_tensor('psA', [128, 2048], F32))
        self.psB = self.g.enter_context(nc.psum_tensor('psB', [128, 2048], F32))
        self.ident = self.gsb('ident', [128, 128])
        self.ident16 = self.gsb('ident16', [128, 128], BF16)
        self.affT = self.gsb('affT', [16, S])
        self.idx = self.gsb('idx', [128, 64], I32)
        self.gb = self.gsb('gb', [128, 2, D])
        self.DMA('sp', self.ident[:], self.d['c_ident'], [], ['ident'])
        self.DMA('pool', self.ident16[:], self.d['c_ident'], [], ['ident16'])
        self.P.barrier()

    def bank(self, j):
        t = self.psA if j < 4 else self.psB
        jj = j % 4
        return t[:, jj * 512:(jj + 1) * 512]

    def bank16(self, j):
        return self.bank(j).bitcast(BF16)

    def load_ln(self, gname, bname, l):
        self.DMA('sp', self.gb[:, 0, :], self.d[gname][l:l + 1, :].to_broadcast([128, D]), [], ['gb'])
        self.DMA('sp', self.gb[:, 1, :], self.d[bname][l:l + 1, :].to_broadcast([128, D]), [], ['gb'])

    def layer_norm(self, y, ykey, xo, xokey, tmp, sfx):
        st, mv, sd, rstd = tmp['st'], tmp['mv'], tmp['sd'], tmp['rstd']
        k = lambda n: n + sfx
        self.OP('dve', 'bn_stats', (st[:, 0:6], y[:, 0:512]), [ykey], [k('st')])
        self.OP('dve', 'bn_stats', (st[:, 6:12], y[:, 512:1024]), [ykey, k('st')], [k('st')])
        self.OP('dve', 'bn_aggr', (mv[:], st[:, 0:12]), [k('st')], [k('mv')])
        self.ACT(sd[:], mv[:, 1:2], AF.Sqrt, [k('mv')], [k('sd')], bias=tmp['eps'][:, 0:1], scale=1.0)
        self.OP('dve', 'reciprocal', (rstd[:], sd[:]), [k('sd')], [k('rstd')])
        self.OP('dve', 'tensor_scalar', (y, y, mv[:, 0:1], rstd[:, 0:1], ALU.subtract, ALU.mult), [ykey, k('mv'), k('rstd')], [ykey])
        self.OP('pool', 'tensor_tensor', (y, y, self.gb[:, 0, :], ALU.mult), [ykey, 'gb'], [ykey])
        self.OP('dve', 'tensor_tensor', (xo, y, self.gb[:, 1, :], ALU.add), [ykey, 'gb'], [xokey])

    def phase_attn(self):
        nc, P, d = self.nc, self.P, self.d
        with contextlib.ExitStack() as es:
            sb = self._sb(es)
            Wq = sb('Wq', [128, 8, 1024], BF16); Wkv = sb('Wkv', [128, 8, 512], BF16); Wo = sb('Wo', [128, 8, 1024], BF16)
            kT = sb('kT', [64, 4, S], BF16); Vr = sb('Vr', [128, NT, 256], BF16)
            cos = sb('cos', [128, NT, 8]); sin = sb('sin', [128, NT, 8])
            maskb = sb('maskb', [128, 384], BF16)
            sink = sb('sink', [128, 16]); nsink = sb('nsink', [128, 16])
            Wr = sb('Wr', [128, 8, NE])
            eps = sb('eps', [128, 1])
            wqkv = d['attn_w_qkv'][0].rearrange('(k p) f -> p k f', p=128)
            self.DMA('pool', Wkv[:], wqkv[:, :, 1024:1536], [], ['Wkv'])
            self.DMA('pool', Wq[:], wqkv[:, :, 0:1024], [], ['Wq'])
            self.DMA('pool', Wo[:], d['attn_w_o'][0].rearrange('(k p) f -> p k f', p=128), [], ['Wo'])
            self.DMA('sp', cos[:], d['c_cos'].rearrange('(i p) j -> p i j', p=128), [], ['cos'])
            self.DMA('sp', sin[:], d['c_sin'].rearrange('(i p) j -> p i j', p=128), [], ['sin'])
            self.DMA('pool', maskb[:], d['c_mask'], [], ['maskb'])
            self.DMA('sp', sink[:], d['attn_sink'][0:1, :].to_broadcast([128, 16]), [], ['sink'])
            self.DMA('sp', Wr[:], d['moe_w_router'][0].rearrange('(k p) e -> p k e', p=128), [], ['Wr'])
            self.load_ln('ln_mix_g', 'ln_mix_b', 0)
            self.OP('dve', 'tensor_scalar', (nsink[:], sink[:], -1.0, None, ALU.mult), ['sink'], ['nsink'])
            self.OP('pool', 'memset', (eps[:], EPS), [], ['eps'])

            xbt = [sb('xbt%d' % i, [128, D], BF16) for i in range(2)]
            xTt = [sb('xTt%d' % i, [128, 8, 128], BF16) for i in range(2)]
            kf = [sb('kf%d' % i, [128, 4, 16]) for i in range(2)]
            kb = [sb('kb%d' % i, [128, 4, 64], BF16) for i in range(2)]
            rt = [sb('rt%d' % i, [128, 4, 16, 8]) for i in range(2)]
            for i in range(NT):
                bi = i % 2
                s_ = str(bi)
                b0, b1, b2 = (0, 1, 2) if bi == 0 else (4, 5, 6)
                tl = slice(i * 128, (i + 1) * 128)
                self.DMA('pool', xbt[bi][:], d['x'][tl, :], [], ['xbt' + s_])
                pT = self.bank16(b0)
                for k in range(8):
                    self.TR(pT[:, k * 128:(k + 1) * 128], xbt[bi][:, k * 128:(k + 1) * 128], self.ident16[:], ['xbt' + s_, 'ident16'], ['b%d' % b0])
                self.OP('dve', 'tensor_copy', (xTt[bi][:], pT.rearrange('p (k t) -> p k t', k=8)), ['b%d' % b0], ['xTt' + s_])
                pkv = self.bank(b1)
                for k in range(8):
                    self.MM(pkv, xTt[bi][:, k, :], Wkv[:, k, :], k == 0, k == 7, ['xTt' + s_, 'Wkv'], ['b%d' % b1])
                pk3 = pkv[:, 0:256].rearrange('p (h e) -> p h e', h=4)
                self.ACT(kb[bi][:], pk3, AF.Copy, ['b%d' % b1], ['kb' + s_])
                self.ACT(kf[bi][:], pk3[:, :, 0:16], AF.Copy, ['b%d' % b1], ['kf' + s_])
                self.ACT(Vr[:, i, :], pkv[:, 256:512], AF.Copy, ['b%d' % b1], [('Vr', i)])
                self.rotary(kf[bi], 'kf' + s_, kb[bi], 'kb' + s_, rt[bi], 'rt' + s_, cos, sin, i, 4)
                pK = self.bank16(b2)
                for h in range(4):
                    self.TR(pK[0:64, h * 128:(h + 1) * 128], kb[bi][:, h, :], self.ident16[:], ['kb' + s_, 'ident16'], ['b%d' % b2])
                self.OP('dve', 'tensor_copy', (kT[:, :, tl], pK[0:64, 0:512].rearrange('p (h t) -> p h t', h=4)), ['b%d' % b2], [('kT', i)])

            xft = [sb('xft%d' % i, [128, D]) for i in range(2)]
            qf = sb('qf', [128, 16, 16]); qb = sb('qb', [128, 16, 64], BF16); rtq = sb('rtq', [128, 4, 16, 8])
            qT = sb('qT', [64, 16, 128], BF16)
            Pb = [sb('Pb%d' % i, [128, 384], BF16) for i in range(2)]
            PT = [sb('PT%d' % i, [128, 384], BF16) for i in range(2)]
            sm = sb('sm', [128, 16, 8])
            rden = sb('rden', [128, 16])
            ot = sb('ot', [128, 16, 64], BF16); oT = sb('oT', [128, 8, 128], BF16)
            y = sb('y', [128, D]); x1 = sb('x1', [128, D]); ya = sb('ya', [128, D]); x1h = sb('x1h', [128, D], BF16)
            lnt = {'st': sb('lnst', [128, 12]), 'mv': sb('lnmv', [128, 2]), 'sd': sb('lnsd', [128, 1]), 'rstd': sb('lnrstd', [128, 1]), 'eps': eps}
            x1T = sb('x1T', [128, 8, 128])
            rs_ = sb('rs_', [128, 8]); ex = sb('ex', [128, NE]); afft = sb('afft', [128, NE])
            for i in range(NT):
                bi = i % 2
                s_ = str(bi)
                tl = slice(i * 128, (i + 1) * 128)
                self.DMA('pool', xbt[bi][:], d['x'][tl, :], [], ['xbt' + s_])
                self.DMA('sp', xft[bi][:], d['x'][tl, :], [], ['xft' + s_])
                pT = self.bank16(0)
                for k in range(8):
                    self.TR(pT[:, k * 128:(k + 1) * 128], xbt[bi][:, k * 128:(k + 1) * 128], self.ident16[:], ['xbt' + s_, 'ident16'], ['b0'])
                self.OP('dve', 'tensor_copy', (xTt[bi][:], pT.rearrange('p (k t) -> p k t', k=8)), ['b0'], ['xTt' + s_])
                pq = self.psA[:, 512:1536]
                for nb in range(2):
                    for k in range(8):
                        self.MM(pq[:, nb * 512:(nb + 1) * 512], xTt[bi][:, k, :], Wq[:, k, nb * 512:(nb + 1) * 512], k == 0, k == 7, ['xTt' + s_, 'Wq'], ['b%d' % (1 + nb)])
                pq3 = pq.rearrange('p (h e) -> p h e', h=16)
                self.ACT(qb[:], pq3, AF.Copy, ['b1', 'b2'], ['qb'])
                self.ACT(qf[:], pq3[:, :, 0:16], AF.Copy, ['b1', 'b2'], ['qf'])
                self.rotary(qf, 'qf', qb, 'qb', rtq, 'rtq', cos, sin, i, 16)
                for half in range(2):
                    bq = 3 + half
                    pQT = self.bank16(bq)
                    for hh in range(8):
                        h = half * 8 + hh
                        self.TR(pQT[0:64, hh * 128:(hh + 1) * 128], qb[:, h, :], self.ident16[:], ['qb', 'ident16'], ['b%d' % bq])
                    eng = 'dve' if half == 0 else 'act'
                    src = pQT[0:64, :].rearrange('p (h t) -> p h t', h=8)
                    if eng == 'dve':
                        self.OP('dve', 'tensor_copy', (qT[:, half * 8:(half + 1) * 8, :], src), ['b%d' % bq], [('qT', half)])
                    else:
                        self.ACT(qT[:, half * 8:(half + 1) * 8, :], src, AF.Copy, ['b%d' % bq], [('qT', half)])
                lo = max(i - 1, 0); hi = min(i + 1, NT - 1)
                nblk = hi - lo + 1; nk = nblk * 128
                m0 = 128 if i == 0 else 0
                pO = self.psB[:, 512:1536]
                for h in range(16):
                    g = h // 4
                    pb = h % 2
                    bs = 1 + pb; bp = 3 + pb
                    pS = self.bank(bs)[:, 0:nk]
                    self.MM(pS, qT[:, h, :], kT[:, g, lo * 128:(hi + 1) * 128], True, False, [('qT', h // 8)] + [('kT', j) for j in range(lo, hi + 1)], ['b%d' % bs])
                    self.MM(pS, self.ident16[:], maskb[:, m0:m0 + nk], False, True, ['ident16', 'maskb'], ['b%d' % bs])
                    mx = sm[:, h, 0:1]; negm = sm[:, h, 1:2]; rs = sm[:, h, 2:3]; es_ = sm[:, h, 3:4]; den = sm[:, h, 4:5]
                    kh = ('sm', h)
                    self.OP('dve', 'reduce_max', (mx, pS, AX.X), ['b%d' % bs], [kh])
                    self.OP('dve', 'tensor_scalar', (negm, mx, -0.125, nsink[:, h:h + 1], ALU.mult, ALU.min), [kh, 'nsink'], [kh])
                    self.ACT(Pb[pb][:, 0:nk], pS, AF.Exp, ['b%d' % bs, kh], ['Pb%d' % pb, kh], bias=negm, scale=0.125, accum_out=rs)
                    self.ACT(es_, negm, AF.Exp, [kh, 'sink'], [kh], bias=sink[:, h:h + 1], scale=1.0)
                    self.OP('dve', 'tensor_tensor', (den, rs, es_, ALU.add), [kh], [kh])
                    self.OP('dve', 'reciprocal', (rden[:, h:h + 1], den), [kh], [('rden', h)])
                    pPT = self.bank16(bp)
                    for b_ in range(nblk):
                        self.TR(pPT[:, b_ * 128:(b_ + 1) * 128], Pb[pb][:, b_ * 128:(b_ + 1) * 128], self.ident16[:], ['Pb%d' % pb, 'ident16'], ['b%d' % bp])
                    self.ACT(PT[pb][:, 0:nk], pPT[:, 0:nk], AF.Copy, ['b%d' % bp], ['PT%d' % pb])
                    for b_ in range(nblk):
                        self.MM(pO[:, h * 64:(h + 1) * 64], PT[pb][:, b_ * 128:(b_ + 1) * 128], Vr[:, lo + b_, g * 64:(g + 1) * 64], b_ == 0, b_ == nblk - 1,
                                ['PT%d' % pb, ('Vr', lo + b_)], ['b%d' % (5 + h // 8)])
                self.OP('dve', 'tensor_tensor', (ot[:], pO.rearrange('p (h e) -> p h e', h=16), rden[:, :].unsqueeze(2).to_broadcast([128, 16, 64]), ALU.mult),
                        ['b5', 'b6'] + [('rden', h) for h in range(16)], ['ot'])
                pOT = self.bank16(0)
                otf = ot[:].rearrange('p h e -> p (h e)')
                for k in range(8):
                    self.TR(pOT[:, k * 128:(k + 1) * 128], otf[:, k * 128:(k + 1) * 128], self.ident16[:], ['ot', 'ident16'], ['b0'])
                self.ACT(oT[:], pOT.rearrange('p (k t) -> p k t', k=8), AF.Copy, ['b0'], ['oT'])
                pM = self.psA[:, 512:1536]
                for nb in range(2):
                    for k in range(8):
                        self.MM(pM[:, nb * 512:(nb + 1) * 512], oT[:, k, :], Wo[:, k, nb * 512:(nb + 1) * 512], k == 0, k == 7, ['oT', 'Wo'], ['b%d' % (1 + nb)])
                self.mixer_tail(i, pM, ['b1', 'b2'], xft[bi], 'xft' + s_, y, x1, ya, x1h, lnt, x1T, Wr, rs_, ex, afft)
            P.barrier()
            P.flush()

    def rotary(self, tf, tfk, tb, tbk, rt, rtk, cos, sin, i, nh):
        c = cos[:, i:i + 1, :].to_broadcast([128, nh, 8])
        s = sin[:, i:i + 1, :].to_broadcast([128, nh, 8])
        t1 = tf[:, :, 0:8]; t2 = tf[:, :, 8:16]
        a = rt[:, 0, 0:nh, :]; b = rt[:, 1, 0:nh, :]; cc = rt[:, 2, 0:nh, :]; dd = rt[:, 3, 0:nh, :]
        self.OP('dve', 'tensor_tensor', (a, t1, c, ALU.mult), [tfk, 'cos'], [rtk + 'a'])
        self.OP('dve', 'tensor_tensor', (b, t2, s, ALU.mult), [tfk, 'sin'], [rtk + 'b'])
        self.OP('dve', 'tensor_tensor', (cc, t2, c, ALU.mult), [tfk, 'cos'], [rtk + 'c'])
        self.OP('dve', 'tensor_tensor', (dd, t1, s, ALU.mult), [tfk, 'sin'], [rtk + 'd'])
        self.OP('dve', 'tensor_tensor', (tb[:, :, 0:8], a, b, ALU.subtract), [rtk + 'a', rtk + 'b', tbk], [tbk])
        self.OP('dve', 'tensor_tensor', (tb[:, :, 8:16], cc, dd, ALU.add), [rtk + 'c', rtk + 'd', tbk], [tbk])

    def mixer_tail(self, i, pM, pMk, xres, xresk, y, x1, ya, x1h, lnt, x1T, Wr, rs_, ex, afft):
        d = self.d
        tl = slice(i * 128, (i + 1) * 128)
        self.OP('dve', 'scalar_tensor_tensor', (y[:], xres[:], ALPHA, pM, ALU.mult, ALU.add), pMk + [xresk], ['y'])
        self.layer_norm(y[:], 'y', x1[:], 'x1', lnt, '')
        self.ACT(ya[:], x1[:], AF.Copy, ['x1'], ['ya'], scale=ALPHA)
        self.OP('pool', 'tensor_copy', (x1h[:], x1[:]), ['x1'], ['x1h'])
        self.DMA('sp', d['ybuf'][tl, :], ya[:], ['ya'], [('ybuf', i)])
        self.DMA('sp', d['x1b'][tl, :], x1h[:], ['x1h'], [('x1b', i)])
        for half in range(2):
            bq = 3 + half
            pX = self.bank(bq)
            for kk in range(4):
                k = half * 4 + kk
                self.TR(pX[:, kk * 128:(kk + 1) * 128], x1[:, k * 128:(k + 1) * 128], self.ident[:], ['x1', 'ident'], ['b%d' % bq])
            src = pX.rearrange('p (k t) -> p k t', k=4)
            if half == 0:
                self.OP('dve', 'tensor_copy', (x1T[:, 0:4, :], src), ['b%d' % bq], [('x1T', 0)])
            else:
                self.ACT(x1T[:, 4:8, :], src, AF.Copy, ['b%d' % bq], [('x1T', 1)])
        pR = self.bank(7)[:, 0:NE]
        for k in range(8):
            self.MM(pR, x1T[:, k, :], Wr[:, k, :], k == 0, k == 7, [('x1T', k // 4), 'Wr'], ['b7'])
        mx = rs_[:, 0:1]; nmx = rs_[:, 1:2]; ssum = rs_[:, 2:3]; rsum = rs_[:, 3:4]
        self.OP('dve', 'reduce_max', (mx, pR, AX.X), ['b7'], ['rs_'])
        self.OP('dve', 'tensor_scalar', (nmx, mx, -1.0, None, ALU.mult), ['rs_'], ['rs_'])
        self.ACT(ex[:], pR, AF.Exp, ['b7', 'rs_'], ['ex', 'rs_'], bias=nmx, scale=1.0, accum_out=ssum)
        self.OP('dve', 'reciprocal', (rsum, ssum), ['rs_'], ['rs_'])
        self.OP('dve', 'tensor_scalar', (afft[:], ex[:], rsum, None, ALU.mult), ['ex', 'rs_'], ['afft'])
        self.DMA('sp', d['aff'][tl, :], afft[:], ['afft'], [('aff', i)])
        pAT = self.bank(7)[0:16, 128:256]
        self.TR(pAT, afft[:], self.ident[:], ['afft', 'ident'], ['b7'])
        self.ACT(self.affT[:, tl], pAT, AF.Copy, ['b7'], [('affT', i)])

    def phase_route(self):
        nc, P, d = self.nc, self.P, self.d
        with contextlib.ExitStack() as es:
            sb = self._sb(es)
            ones = sb('ones', [16, S]); mask = sb('mask', [16, S]); c = sb('c', [16, S]); junk16 = sb('junk16', [16, S], BF16)
            bsv = sb('bsv', [16, 8])
            sel = sb('sel', [16, 2048]); slot = sb('slot', [128, 4])
            junk = [sb('junk%d' % i, [128, 2048], BF16) for i in range(4)]
            cntp = sb('cntp', [128, 64, 2]); idxf = sb('idxf', [128, 64])
            self.DMA('sp', sel[:], d['c_sel'], [], ['sel'])
            self.DMA('sp', slot[:], d['c_slot'], [], ['slot'])
            self.OP('pool', 'memset', (ones[:], 1.0), [], ['ones'])
            lo = bsv[:, 0:1]; hi = bsv[:, 1:2]; mid = bsv[:, 2:3]; cnt = bsv[:, 3:4]; ge = bsv[:, 4:5]; nge = bsv[:, 5:6]; t2 = bsv[:, 6:7]
            affk = [('affT', i) for i in range(NT)]
            self.OP('dve', 'memset', (lo, 0.0), [], ['lo'])
            self.OP('dve', 'memset', (hi, 2.0), [], ['hi'])
            for it in range(31):
                self.OP('dve', 'tensor_scalar', (mid, lo, hi, 0.5, ALU.add, ALU.mult), ['lo', 'hi'], ['mid'])
                self.OP('dve', 'tensor_scalar', (junk16[:], self.affT[:], mid, None, ALU.is_ge, ALU.add), affk + ['mid'], ['junk16', 'cnt'], accum_out=cnt)
                self.OP('dve', 'tensor_scalar', (ge, cnt, float(CAP), None, ALU.is_ge), ['cnt'], ['ge'])
                self.OP('dve', 'tensor_scalar', (nge, cnt, float(CAP), None, ALU.is_lt), ['cnt'], ['nge'])
                self.OP('dve', 'scalar_tensor_tensor', (lo, mid, ge, lo, ALU.mult, ALU.max), ['mid', 'ge', 'lo'], ['lo'])
                self.OP('dve', 'tensor_tensor', (t2, hi, ge, ALU.mult), ['hi', 'ge'], ['t2'])
                self.OP('dve', 'scalar_tensor_tensor', (hi, mid, nge, t2, ALU.mult, ALU.add), ['mid', 'nge', 't2'], ['hi'])
            self.OP('dve', 'tensor_scalar', (mask[:], self.affT[:], lo, None, ALU.is_ge), affk + ['lo'], ['mask'])
            self.OP('dve', 'tensor_tensor_scan', (c[:], ones[:], mask[:], 0.0, ALU.mult, ALU.add), ['ones', 'mask'], ['c'])
            for e in range(NE):
                for hf in range(2):
                    ps = self.psA if hf == 0 else self.psB
                    bk = ['b%d' % (hf * 4 + j) for j in range(4)]
                    for j in range(4):
                        self.MM(ps[:, j * 512:(j + 1) * 512], sel[:, e * 128:(e + 1) * 128], c[:, hf * 2048 + j * 512: hf * 2048 + (j + 1) * 512], True, True, ['sel', 'c'], [bk[j]])
                    for st in range(4):
                        col = e * 4 + st
                        self.OP('dve', 'tensor_scalar', (junk[st][:], ps[:, :], slot[:, st:st + 1], None, ALU.is_le, ALU.add), bk + ['slot'], ['junk%d' % st, ('cntp', col, hf)],
                                accum_out=cntp[:, col, hf:hf + 1])
            self.OP('dve', 'tensor_tensor', (idxf[:], cntp[:, :, 0], cntp[:, :, 1], ALU.add), [('cntp', cc, hh) for cc in range(64) for hh in range(2)], ['idxf'])
            self.OP('dve', 'tensor_copy', (self.idx[:], idxf[:]), ['idxf'], ['idx'])
            if self.dbg:
                self.DMA('sp', d['dbg_idx'], self.idx[:], ['idx'], ['dbg_idx'])
            P.barrier()
            P.flush()

    def phase_experts(self, l):
        nc, P, d = self.nc, self.P, self.d
        with contextlib.ExitStack() as es:
            sb = self._sb(es)
            Wgu = [sb('Wgu%d' % i, [128, 8, 2048], BF16) for i in range(2)]
            Wd = [sb('Wd%d' % i, [128, 8, 1024], BF16) for i in range(2)]
            xs = [[sb('xs%d_%d' % (i, st), [128, D], BF16) for st in range(4)] for i in range(2)]
            gt = [[sb('gt%d_%d' % (i, st), [128, NE]) for st in range(4)] for i in range(2)]
            xsT = [sb('xsT%d' % i, [128, 8, 512], BF16) for i in range(2)]
            sg = [sb('sg%d' % i, [128, 512]) for i in range(2)]
            actT = sb('actT', [128, 8, 512], BF16)
            orow = [sb('orow%d' % i, [128, D]) for i in range(4)]

            def gather(out, src, col, wkey):
                P.dma('pool', lambda e: e.indirect_dma_start(out=out, out_offset=None, in_=src,
                                                             in_offset=bass.IndirectOffsetOnAxis(ap=self.idx[:, col:col + 1], axis=0)),
                      ['idx'], [wkey])

            def stageA(e):
                p_ = e % 2
                self.DMA('pool', Wgu[p_][:], d['moe_w_gate_up'][l, e].rearrange('(k p) f -> p k f', p=128), [], ['Wgu%d' % p_])
                self.DMA('pool', Wd[p_][:], d['moe_w_down'][l, e].rearrange('(k p) f -> p k f', p=128), [], ['Wd%d' % p_])
                for st in range(4):
                    gather(xs[p_][st][:], d['x1b'], e * 4 + st, ('xs', p_, st))
                    gather(gt[p_][st][:], d['aff'], e * 4 + st, ('gt', p_, st))

            def stageB(e):
                p_ = e % 2
                for st in range(4):
                    b = st % 2
                    pT = self.bank16(b)
                    for k in range(8):
                        self.TR(pT[:, k * 128:(k + 1) * 128], xs[p_][st][:, k * 128:(k + 1) * 128], self.ident16[:], [('xs', p_, st), 'ident16'], ['b%d' % b])
                    dst = xsT[p_][:, :, st * 128:(st + 1) * 128]
                    src = pT.rearrange('p (k t) -> p k t', k=8)
                    if st % 2 == 0:
                        self.OP('dve', 'tensor_copy', (dst, src), ['b%d' % b], [('xsT', p_, st)])
                    else:
                        self.ACT(dst, src, AF.Copy, ['b%d' % b], [('xsT', p_, st)])
                xk = [('xsT', p_, st) for st in range(4)]
                for i in range(8):
                    bg = 2 + (i % 2) * 2; bu = bg + 1
                    pG = self.bank(bg); pU = self.bank(bu)
                    for k in range(8):
                        self.MM(pG, Wgu[p_][:, k, i * 128:(i + 1) * 128], xsT[p_][:, k, :], k == 0, k == 7, ['Wgu%d' % p_] + xk, ['b%d' % bg])
                    for k in range(8):
                        self.MM(pU, Wgu[p_][:, k, 1024 + i * 128:1024 + (i + 1) * 128], xsT[p_][:, k, :], k == 0, k == 7, ['Wgu%d' % p_] + xk, ['b%d' % bu])
                    self.ACT(sg[i % 2][:], pG, AF.Silu, ['b%d' % bg], ['sg%d' % (i % 2)])
                    self.OP('dve', 'tensor_tensor', (actT[:, i, :], sg[i % 2][:], pU, ALU.mult), ['sg%d' % (i % 2), 'b%d' % bu], [('actT', i)])
                ak = [('actT', i) for i in range(8)]
                for st in range(4):
                    gate = gt[p_][st][:, e:e + 1]
                    for nb in range(2):
                        bd = 6 + nb
                        pD = self.bank(bd)
                        for f in range(8):
                            self.MM(pD, actT[:, f, st * 128:(st + 1) * 128], Wd[p_][:, f, nb * 512:(nb + 1) * 512], f == 0, f == 7, ak + ['Wd%d' % p_], ['b%d' % bd])
                        dst = orow[st][:, nb * 512:(nb + 1) * 512]
                        if nb == 0:
                            self.ACT(dst, pD, AF.Copy, ['b%d' % bd, ('gt', p_, st)], [('orow', st, nb)], scale=gate)
                        else:
                            self.OP('dve', 'tensor_scalar', (dst, pD, gate, None, ALU.mult), ['b%d' % bd, ('gt', p_, st)], [('orow', st, nb)])
                    col = e * 4 + st
                    P.dma('pool', (lambda o_, c_: lambda en: en.indirect_dma_start(out=d['ybuf'], out_offset=bass.IndirectOffsetOnAxis(ap=self.idx[:, c_:c_ + 1], axis=0),
                                                                                     in_=o_, in_offset=None, compute_op=ALU.add))(orow[st][:], col),
                          [('orow', st, 0), ('orow', st, 1), 'idx', 'ybuf'], ['ybuf'])

            stageA(0)
            for e in range(NE):
                if e + 1 < NE:
                    stageA(e + 1)
                stageB(e)
            P.barrier()
            P.flush()

    def phase_lnout(self, l):
        nc, P, d = self.nc, self.P, self.d
        with contextlib.ExitStack() as es:
            sb = self._sb(es)
            eps = sb('eps2', [128, 1])
            self.OP('pool', 'memset', (eps[:], EPS), [], ['eps'])
            self.load_ln('ln_ffn_g', 'ln_ffn_b', l)
            yt = [sb('yt%d' % i, [128, D]) for i in range(2)]
            xo = [sb('xo%d' % i, [128, D]) for i in range(2)]
            xh = [sb('xh%d' % i, [128, D], BF16) for i in range(2)]
            xTt = [sb('x2Tt%d' % i, [128, 8, 128], BF16) for i in range(2)]
            lnt = [{'st': sb('lnst%d' % i, [128, 12]), 'mv': sb('lnmv%d' % i, [128, 2]), 'sd': sb('lnsd%d' % i, [128, 1]), 'rstd': sb('lnrstd%d' % i, [128, 1]), 'eps': eps} for i in range(2)]
            x2T3 = d['x2T'].rearrange('(k p) t -> p k t', p=128)
            for i in range(NT):
                bi = i % 2
                s_ = str(bi)
                tl = slice(i * 128, (i + 1) * 128)
                self.DMA('sp', yt[bi][:], d['ybuf'][tl, :], [], ['yt' + s_])
                self.layer_norm(yt[bi][:], 'yt' + s_, xo[bi][:], 'xo' + s_, lnt[bi], s_)
                if l == 0:
                    self.DMA('sp', d['x2buf'][tl, :], xo[bi][:], ['xo' + s_], [('x2buf', i)])
                    self.ACT(xh[bi][:], xo[bi][:], AF.Copy, ['xo' + s_], ['xh' + s_])
                    pT = self.bank16(bi)
                    for k in range(8):
                        self.TR(pT[:, k * 128:(k + 1) * 128], xh[bi][:, k * 128:(k + 1) * 128], self.ident16[:], ['xh' + s_, 'ident16'], ['b%d' % bi])
                    self.ACT(xTt[bi][:], pT.rearrange('p (k t) -> p k t', k=8), AF.Copy, ['b%d' % bi], ['x2Tt' + s_])
                    self.DMA('sp', x2T3[:, :, tl], xTt[bi][:], ['x2Tt' + s_], [('x2T', i)])
                else:
                    self.DMA('sp', d['out'][tl, :], xo[bi][:], ['xo' + s_], [('out', i)])
            P.barrier()
            P.flush()

    def phase_lru(self):
        nc, P, d = self.nc, self.P, self.d
        with contextlib.ExitStack() as es:
            sb = self._sb(es)
            BIG = sb('BIG', [128, 2, 4100])
            xc = sb('xc', [128, 2, S]); xcb = sb('xcb', [128, 2, S], BF16); gg = sb('gg', [128, 2, S], BF16)
            M = sb('M', [128, S]); Hf = sb('Hf', [128, S]); hgb = sb('hgb', [128, S], BF16)
            Win = [sb('Win%d' % i, [128, 8, 512], BF16) for i in range(2)]
            Wg = [sb('Wg%d' % i, [128, 4, 2, 256], BF16) for i in range(2)]
            x2Tb = [sb('x2Tb%d' % i, [128, 8, 512], BF16) for i in range(2)]
            prow = sb('prow', [88, 128]); prm = sb('prm', [128, 88])
            sp_ = sb('sp_', [128, 12, 16]); kf = sb('kfl', [128, 16]); one = sb('one', [128, 1])
            self.DMA('sp', prow[0:32, :], d['lru_conv_w'][0].rearrange('j (c p) -> (j c) p', p=128), [], ['prow'])
            self.DMA('sp', prow[32:40, :], d['lru_conv_b'][0].rearrange('(c p) -> c p', p=128), [], ['prow'])
            self.DMA('sp', prow[40:56, :], d['lru_b_rgate'][0].rearrange('z (c p) -> (z c) p', p=128), [], ['prow'])
            self.DMA('sp', prow[56:72, :], d['lru_b_igate'][0].rearrange('z (c p) -> (z c) p', p=128), [], ['prow'])
            self.DMA('sp', prow[72:88, :], d['lru_lambda'][0].rearrange('z (c p) -> (z c) p', p=128), [], ['prow'])
            pp = self.bank(0)[:, 0:88]
            self.TR(pp, prow[:], self.ident[0:88, 0:88], ['prow', 'ident'], ['b0'])
            self.OP('dve', 'tensor_copy', (prm[:], pp), ['b0'], ['prm'])
            self.OP('pool', 'memset', (one[:], 1.0), [], ['one'])
            cw = lambda j, gc: prm[:, j * 8 + gc: j * 8 + gc + 1]
            cb = lambda gc: prm[:, 32 + gc: 33 + gc]
            br = lambda z, gc: prm[:, 40 + z * 8 + gc: 41 + z * 8 + gc]
            bi_ = lambda z, gc: prm[:, 56 + z * 8 + gc: 57 + z * 8 + gc]
            lam = prm[:, 72:88]
            T = lambda i: sp_[:, i, :]
            V = lambda meth, args: self.OP('dve', meth, args, ['prm', 'spk'], ['spk'])
            V('tensor_scalar', (T(10), lam, -1.0, None, ALU.mult))
            V('tensor_tensor', (T(0), lam, T(10), ALU.max))
            self.ACT(T(1), T(0), AF.Exp, ['spk'], ['spk'], scale=-1.0)
            V('tensor_scalar', (T(2), T(1), 2.0, None, ALU.add))
            V('reciprocal', (T(3), T(2)))
            V('tensor_tensor', (T(4), T(1), T(3), ALU.mult))
            V('tensor_tensor', (T(5), T(4), T(4), ALU.mult))
            V('tensor_scalar', (T(6), T(5), 1.0 / 11.0, 1.0 / 9.0, ALU.mult, ALU.add))
            for cst in (1.0 / 7.0, 1.0 / 5.0, 1.0 / 3.0, 1.0):
                V('tensor_tensor', (T(6), T(6), T(5), ALU.mult))
                V('tensor_scalar', (T(6), T(6), cst, None, ALU.add))
            V('tensor_tensor', (T(7), T(4), T(6), ALU.mult))
            V('tensor_scalar', (T(8), lam, -1.0, 0.0, ALU.mult, ALU.max))
            V('scalar_tensor_tensor', (T(9), T(7), 2.0, T(8), ALU.mult, ALU.add))
            self.OP('dve', 'tensor_scalar', (kf[:], T(9), -8.0, None, ALU.mult), ['spk'], ['kf'])
            kfc = lambda z, gc: kf[:, z * 8 + gc: z * 8 + gc + 1]

            x2T3 = d['x2T'].rearrange('(k p) t -> p k t', p=128)
            win = d['lru_w_in'][0].rearrange('(k p) f -> p k f', p=128)
            A = BIG[:, 0, 0:S]; I_ = BIG[:, 1, 0:S]
            for n in range(4):
                p_ = n % 2
                self.DMA('pool', Win[p_][:, :, 0:256], win[:, :, n * 256:(n + 1) * 256], [], [('Win', p_, 0)])
                self.DMA('pool', Win[p_][:, :, 256:512], win[:, :, 1024 + n * 256:1024 + (n + 1) * 256], [], [('Win', p_, 1)])
                for z in range(2):
                    self.DMA('pool', Wg[p_][:, z * 2 + 0, :, :], d['lru_w_rgate'][0, z, n].rearrange('(c p) f -> p c f', p=128), [], [('Wg', p_, z * 2)])
                    self.DMA('pool', Wg[p_][:, z * 2 + 1, :, :], d['lru_w_igate'][0, z, n].rearrange('(c p) f -> p c f', p=128), [], [('Wg', p_, z * 2 + 1)])
                for cc in range(2):
                    self.OP('pool', 'memset', (BIG[:, cc, 0:2], 0.0), [], [('big', cc)])
                    self.OP('pool', 'memset', (BIG[:, cc, 4098:4100], 0.0), [('big', cc)], [('big', cc)])
                for tb in range(8):
                    tq = tb % 2
                    ts_ = slice(tb * 512, (tb + 1) * 512)
                    self.DMA('sp', x2Tb[tq][:], x2T3[:, :, ts_], [], ['x2Tb%d' % tq])
                    for j in range(4):
                        bk = j + 4 * tq
                        ps = self.bank(bk)
                        for k in range(8):
                            self.MM(ps, Win[p_][:, k, j * 128:(j + 1) * 128], x2Tb[tq][:, k, :], k == 0, k == 7, [('Win', p_, j // 2), 'x2Tb%d' % tq], ['b%d' % bk])
                        if j < 2:
                            self.ACT(gg[:, j, ts_], ps, AF.Gelu_apprx_tanh, ['b%d' % bk], [('gg', j)])
                        else:
                            self.OP('dve', 'tensor_copy', (BIG[:, j - 2, 2 + tb * 512: 2 + (tb + 1) * 512], ps), ['b%d' % bk, ('big', j - 2)], [('big', j - 2)])
                for cc in range(2):
                    gc = n * 2 + cc
                    self.OP('dve', 'tensor_scalar', (xc[:, cc, :], BIG[:, cc, 0:S], cw(0, gc), cb(gc), ALU.mult, ALU.add), [('big', cc), 'prm'], [('xc', cc)])
                    for j in range(1, 4):
                        self.OP('dve', 'scalar_tensor_tensor', (xc[:, cc, :], BIG[:, cc, j:j + S], cw(j, gc), xc[:, cc, :], ALU.mult, ALU.add), [('big', cc), 'prm', ('xc', cc)], [('xc', cc)])
                    self.OP('pool', 'tensor_copy', (xcb[:, cc, :], xc[:, cc, :]), [('xc', cc)], [('xcb', cc)])
                for fc in range(2):
                    gf = n * 2 + fc
                    for z in range(2):
                        for tb in range(8):
                            ts_ = slice(tb * 512, (tb + 1) * 512)
                            br_ = (tb % 2) * 2; bi2 = br_ + 1
                            pr = self.bank(br_); pi = self.bank(bi2)
                            for cc in range(2):
                                self.MM(pr, Wg[p_][:, z * 2, cc, fc * 128:(fc + 1) * 128], xcb[:, cc, ts_], cc == 0, cc == 1, [('Wg', p_, z * 2), ('xcb', cc)], ['b%d' % br_])
                            self.ACT(A[:, ts_], pr, AF.Sigmoid, ['b%d' % br_, 'prm', ('big', 0)], [('big', 0)], bias=br(z, gf), scale=1.0)
                            for cc in range(2):
                                self.MM(pi, Wg[p_][:, z * 2 + 1, cc, fc * 128:(fc + 1) * 128], xcb[:, cc, ts_], cc == 0, cc == 1, [('Wg', p_, z * 2 + 1), ('xcb', cc)], ['b%d' % bi2])
                            self.ACT(I_[:, ts_], pi, AF.Sigmoid, ['b%d' % bi2, 'prm', ('big', 1)], [('big', 1)], bias=bi_(z, gf), scale=1.0)
                        self.ACT(A, A, AF.Exp, [('big', 0), 'kf'], [('big', 0)], scale=kfc(z, gf))
                        self.OP('pool', 'tensor_tensor', (M[:], A, A, ALU.mult), [('big', 0)], ['M'])
                        self.ACT(M[:], M[:], AF.Sqrt, ['M', 'one'], ['M'], bias=one[:, 0:1], scale=-1.0)
                        sc = 0 if z == 0 else S - 1
                        self.OP('dve', 'memset', (M[:, sc:sc + 1], 1.0), ['M'], ['M'])
                        self.OP('dve', 'tensor_tensor', (I_, I_, M[:], ALU.mult), [('big', 1), 'M'], [('big', 1)])
                        self.OP('dve', 'tensor_tensor', (I_, I_, xc[:, fc, :], ALU.mult), [('big', 1), ('xc', fc)], [('big', 1)])
                        if z == 0:
                            self.OP('dve', 'tensor_tensor_scan', (Hf[:], A, I_, 0.0, ALU.mult, ALU.add), [('big', 0), ('big', 1)], ['Hf'])
                        else:
                            self.OP('dve', 'tensor_tensor_scan', (M[:, ::-1], A[:, ::-1], I_[:, ::-1], 0.0, ALU.mult, ALU.add), [('big', 0), ('big', 1), 'M'], ['M'])
                            self.OP('pool', 'tensor_tensor', (Hf[:], Hf[:], M[:], ALU.add), ['Hf', 'M'], ['Hf'])
                            self.OP('dve', 'tensor_tensor', (hgb[:], Hf[:], gg[:, fc, :], ALU.mult), ['Hf', ('gg', fc)], ['hgb'])
                            self.DMA('sp', d['hgT'][gf * 128:(gf + 1) * 128, :], hgb[:], ['hgb'], [('hgT', gf)])
            P.barrier()
            P.flush()

    def phase_lru_tail(self):
        nc, P, d = self.nc, self.P, self.d
        with contextlib.ExitStack() as es:
            sb = self._sb(es)
            Wout = sb('Wout', [128, 8, D], BF16); Wr = sb('Wr1', [128, 8, NE]); eps = sb('eps3', [128, 1])
            hgTb = [sb('hgTb%d' % i, [128, 8, 512], BF16) for i in range(2)]
            xft = [sb('xft_%d' % i, [128, D]) for i in range(2)]
            y = sb('y_', [128, D]); x1 = sb('x1_', [128, D]); ya = sb('ya_', [128, D]); x1h = sb('x1h_', [128, D], BF16)
            lnt = {'st': sb('lnst_', [128, 12]), 'mv': sb('lnmv_', [128, 2]), 'sd': sb('lnsd_', [128, 1]), 'rstd': sb('lnrstd_', [128, 1]), 'eps': eps}
            x1T = sb('x1T_', [128, 8, 128]); rs_ = sb('rs__', [128, 8]); ex = sb('ex_', [128, NE]); afft = sb('afft_', [128, NE])
            self.DMA('pool', Wout[:], d['lru_w_out'][0].rearrange('(k p) f -> p k f', p=128), [], ['Wout'])
            self.DMA('sp', Wr[:], d['moe_w_router'][1].rearrange('(k p) e -> p k e', p=128), [], ['Wr'])
            self.OP('pool', 'memset', (eps[:], EPS), [], ['eps'])
            self.load_ln('ln_mix_g', 'ln_mix_b', 1)
            hg3 = d['hgT'].rearrange('(k p) t -> p k t', p=128)
            for i in range(NT):
                bi = i % 2
                s_ = str(bi)
                tl = slice(i * 128, (i + 1) * 128)
                blk = i // 4; bq = blk % 2
                if i % 4 == 0:
                    self.DMA('sp', hgTb[bq][:], hg3[:, :, blk * 512:(blk + 1) * 512], [], ['hgTb%d' % bq])
                self.DMA('sp', xft[bi][:], d['x2buf'][tl, :], [], ['xft' + s_])
                pM = self.psA[:, 512:1536]
                for nb in range(2):
                    for k in range(8):
                        self.MM(pM[:, nb * 512:(nb + 1) * 512], hgTb[bq][:, k, (i % 4) * 128:(i % 4 + 1) * 128], Wout[:, k, nb * 512:(nb + 1) * 512], k == 0, k == 7,
                                ['hgTb%d' % bq, 'Wout'], ['b%d' % (1 + nb)])
                self.mixer_tail(i, pM, ['b1', 'b2'], xft[bi], 'xft' + s_, y, x1, ya, x1h, lnt, x1T, Wr, rs_, ex, afft)
            P.barrier()
            P.flush()

    def finish(self):
        self.P.barrier()
        self.P.flush()
        self.g.close()
        self.P.es.close()


def make_consts():
    c = {}
    c['c_ident'] = np.eye(128, dtype=np.float32)
    pos = np.arange(S, dtype=np.float32)
    inv = (np.float32(500000.0) ** (-np.arange(0, 16, 2, dtype=np.float32) / np.float32(16))).astype(np.float32)
    ang = (pos[:, None] * inv[None, :]).astype(np.float32)
    c['c_cos'] = np.cos(ang).astype(np.float32)
    c['c_sin'] = np.sin(ang).astype(np.float32)
    q = np.arange(128)[:, None]; k = np.arange(128)[None, :]
    m = np.zeros((128, 3, 128), np.float32)
    m[:, 0, :] = np.where(k >= q, 0.0, NEG)
    m[:, 2, :] = np.where(k <= q, 0.0, NEG)
    c['c_mask'] = m.reshape(128, 384)
    c['c_slot'] = (np.arange(4)[None, :] * 128 + np.arange(128)[:, None]).astype(np.float32) + np.float32(0.5)
    sel = np.zeros((16, 16, 128), np.float32)
    for e in range(16):
        sel[e, e, :] = 1.0
    c['c_sel'] = sel.reshape(16, 2048)
    return c


_CACHE = {}


def build(dbg=False, stop=99):
    key = (dbg, stop)
    if key in _CACHE:
        return _CACHE[key]
    k = K(dbg, stop)
    k.declare()
    k.phase_attn()
    if stop >= 2:
        k.phase_route()
    if stop >= 3:
        k.phase_experts(0)
    if stop >= 4:
        k.phase_lnout(0)
    if stop >= 5:
        k.phase_lru()
    if stop >= 6:
        k.phase_lru_tail()
    if stop >= 7:
        k.phase_route()
        k.phase_experts(1)
        k.phase_lnout(1)
    k.finish()
    _CACHE[key] = k
    return k


WEIGHTS = ['attn_w_qkv', 'attn_w_o', 'attn_sink', 'lru_w_in', 'lru_conv_w', 'lru_conv_b', 'lru_w_rgate', 'lru_b_rgate',
           'lru_w_igate', 'lru_b_igate', 'lru_lambda', 'lru_w_out', 'moe_w_router', 'moe_w_gate_up', 'moe_w_down',
           'ln_mix_g', 'ln_mix_b', 'ln_ffn_g', 'ln_ffn_b']


def kernel(**inputs):
    k = build()
    consts = make_consts()
    x = np.ascontiguousarray(inputs['x'], dtype=np.float32)
    base = {n: np.ascontiguousarray(inputs[n], dtype=np.float32) for n in WEIGHTS}
    base.update(consts)
    in_maps = []
    for c in range(8):
        m = dict(base)
        m['x'] = x[c % 4]
        in_maps.append(m)
    res = run_bass_kernel_spmd(k.nc, in_maps, core_ids=list(range(8)))
    return np.stack([res.results[c]['out'] for c in range(4)], axis=0).astype(np.float32)
```
